# Optimizing a Trainium2 kernel written in Bass

```python
import jax, jax.numpy as jnp
from jax import lax
import numpy as np

D_MODEL = 1024
BATCH = 8
SEQ = 4096
DEPTH = 4

GRID_W = 64
CTX_LEN = 256
N_MIXERS = 2
NA_HEADS = 16
NA_HEAD_DIM = D_MODEL // NA_HEADS
NA_KH_MAX = 8
NA_KW = 16
NA_KB = 2 * NA_KW
NA_NCB = GRID_W // NA_KW
SG_CHUNK = 128
SG_HALF = 3 * D_MODEL
SG_GROUP_CH = 128
SG_GROUPS = SG_HALF // SG_GROUP_CH
MLP_HIDDEN = 4 * D_MODEL
N_ATTN_LAYERS = (DEPTH + N_MIXERS - 1) // N_MIXERS
N_SG_LAYERS = DEPTH // N_MIXERS
EPS = 1e-6
NEG_INF = -1e30

kernel_name = "hybrid_natten_gmlp_dit_prefix"


def rms_norm(t, g):
    t32 = t.astype(jnp.float32)
    t32 = t32 * lax.rsqrt(jnp.mean(t32 * t32, axis=-1, keepdims=True) + EPS)
    return (t32 * g.astype(jnp.float32)).astype(t.dtype)


def layer_norm(t, g, b):
    t32 = t.astype(jnp.float32)
    mu = jnp.mean(t32, axis=-1, keepdims=True)
    var = jnp.mean(jnp.square(t32 - mu), axis=-1, keepdims=True)
    out = (t32 - mu) * lax.rsqrt(var + EPS) * g.astype(jnp.float32) + b.astype(jnp.float32)
    return out.astype(t.dtype)


def modulate(h, shift, scale):
    return h * (1.0 + scale) + shift


def squared_relu_mlp(h, w1, w2):
    return jnp.square(jax.nn.relu(h @ w1)) @ w2


def _na_column_tables():
    j = np.arange(NA_NCB)[:, None, None]
    qi = np.arange(NA_KW)[None, :, None]
    m = np.arange(NA_KB)[None, None, :]
    qcol = j * NA_KW + qi
    kstart = np.clip(j * NA_KW - NA_KW // 2, 0, GRID_W - NA_KB)
    kcol = kstart + m
    wstart = np.clip(qcol - NA_KW // 2, 0, GRID_W - NA_KW)
    valid = (kcol >= wstart) & (kcol < wstart + NA_KW)
    col_rel = np.clip(kcol - qcol + NA_KW - 1, 0, 2 * NA_KW - 2)
    key_cols = np.clip(np.arange(NA_NCB) * NA_KW - NA_KW // 2, 0, GRID_W - NA_KB)[:, None] + np.arange(NA_KB)[None, :]
    return key_cols, valid, col_rel


def neighbourhood_attention(h, hc, w_qkv, q_g, k_g, rpb, w_o, ctx_out):
    bsz, n_tok, _ = h.shape
    rows = n_tok // GRID_W
    kh = min(NA_KH_MAX, rows)
    scale = NA_HEAD_DIM ** -0.5

    def heads(t):
        return t.reshape(t.shape[0], t.shape[1], NA_HEADS, NA_HEAD_DIM)

    q, k, v = jnp.split(h @ w_qkv, 3, axis=-1)
    q = rms_norm(heads(q), q_g)
    k = rms_norm(heads(k), k_g)
    v = heads(v)
    if ctx_out:
        qc, kc, vc = jnp.split(hc @ w_qkv, 3, axis=-1)
        qc = rms_norm(heads(qc), q_g)
    else:
        kc, vc = jnp.split(hc @ w_qkv[:, D_MODEL:], 2, axis=-1)
    kc = rms_norm(heads(kc), k_g)
    vc = heads(vc)

    qg = q.reshape(bsz, rows, NA_NCB, NA_KW, NA_HEADS, NA_HEAD_DIM)
    kg = k.reshape(bsz, rows, GRID_W, NA_HEADS, NA_HEAD_DIM)
    vg = v.reshape(bsz, rows, GRID_W, NA_HEADS, NA_HEAD_DIM)
    key_cols, valid, col_rel = _na_column_tables()
    n_lat_keys = kh * NA_KB

    def row_step(r):
        rs = jnp.clip(r - kh // 2, 0, rows - kh)
        q_r = lax.dynamic_index_in_dim(qg, r, axis=1, keepdims=False)
        k_r = lax.dynamic_slice_in_dim(kg, rs, kh, axis=1)[:, :, key_cols]
        v_r = lax.dynamic_slice_in_dim(vg, rs, kh, axis=1)[:, :, key_cols]
        row_rel = rs + jnp.arange(kh) - r + NA_KH_MAX - 1
        bias = rpb[:, row_rel][:, :, col_rel]
        bias = jnp.where(valid[None, None], bias, NEG_INF).transpose(0, 2, 3, 1, 4)
        s_lat = jnp.einsum('bjqhd,bkjmhd->bhjqkm', q_r, k_r,
                           preferred_element_type=jnp.float32) * scale + bias[None].astype(jnp.float32)
        s_ctx = jnp.einsum('bjqhd,blhd->bhjql', q_r, kc,
                           preferred_element_type=jnp.float32) * scale
        s = jnp.concatenate([s_lat.reshape(s_lat.shape[:4] + (n_lat_keys,)), s_ctx], axis=-1)
        p = jax.nn.softmax(s, axis=-1).astype(v.dtype)
        p_lat = p[..., :n_lat_keys].reshape(s_lat.shape)
        p_ctx = p[..., n_lat_keys:]
        o = (jnp.einsum('bhjqkm,bkjmhd->bjqhd', p_lat, v_r)
             + jnp.einsum('bhjql,blhd->bjqhd', p_ctx, vc))
        return o.reshape(bsz, GRID_W, D_MODEL)

    y = lax.map(row_step, jnp.arange(rows))
    y = y.transpose(1, 0, 2, 3).reshape(bsz, n_tok, D_MODEL) @ w_o
    yc = None
    if ctx_out:
        sc = jnp.einsum('blhd,bmhd->bhlm', qc, kc, preferred_element_type=jnp.float32) * scale
        pc = jax.nn.softmax(sc, axis=-1).astype(vc.dtype)
        yc = jnp.einsum('bhlm,bmhd->blhd', pc, vc).reshape(bsz, -1, D_MODEL) @ w_o
    return y, yc


def spatial_gating(h, w_in, b_in, ln_g, ln_b, w_s, b_s, w_o):
    bsz, n, _ = h.shape
    z = jax.nn.gelu(h @ w_in + b_in, approximate=False)
    u, v = jnp.split(z, 2, axis=-1)
    v = layer_norm(v, ln_g, ln_b)
    v = v.reshape(bsz, n // SG_CHUNK, SG_CHUNK, SG_GROUPS, SG_GROUP_CH)
    s = jnp.einsum('gpq,bnqgc->bnpgc', w_s, v) + b_s.T[None, None, :, :, None]
    return (u * s.reshape(bsz, n, SG_HALF)) @ w_o


def setup_inputs(seed: int = 0) -> dict:
    key = jax.random.key(seed)
    ks = jax.random.split(key, 24)
    f32 = jnp.float32
    nrm = lambda k, shape, s: jax.random.normal(k, shape, f32) * s
    D = D_MODEL
    return {
        "x": nrm(ks[0], (BATCH, SEQ, D), 1.0),
        "c": nrm(ks[1], (BATCH, D), 1.0),
        "ctx": nrm(ks[2], (BATCH, CTX_LEN, D), 1.0),
        "c_ctx": nrm(ks[3], (D,), 1.0),
        "ada_w": nrm(ks[4], (DEPTH, D, 6 * D), 0.5 * D ** -0.5),
        "ada_b": nrm(ks[5], (DEPTH, 6 * D), 0.02),
        "norm1_g": 1.0 + nrm(ks[6], (DEPTH, D), 0.02),
        "norm2_g": 1.0 + nrm(ks[7], (DEPTH, D), 0.02),
        "mlp_w1": nrm(ks[8], (DEPTH, D, MLP_HIDDEN), D ** -0.5),
        "mlp_w2": nrm(ks[9], (DEPTH, MLP_HIDDEN, D), MLP_HIDDEN ** -0.5),
        "na_w_qkv": nrm(ks[10], (N_ATTN_LAYERS, D, 3 * D), D ** -0.5),
        "na_q_norm": 1.0 + nrm(ks[11], (N_ATTN_LAYERS, NA_HEAD_DIM), 0.02),
        "na_k_norm": 1.0 + nrm(ks[12], (N_ATTN_LAYERS, NA_HEAD_DIM), 0.02),
        "na_rpb": nrm(ks[13], (N_ATTN_LAYERS, NA_HEADS, 2 * NA_KH_MAX - 1, 2 * NA_KW - 1), 0.02),
        "na_w_o": nrm(ks[14], (N_ATTN_LAYERS, D, D), D ** -0.5),
        "sg_w_in": nrm(ks[15], (N_SG_LAYERS, D, 2 * SG_HALF), D ** -0.5),
        "sg_b_in": nrm(ks[16], (N_SG_LAYERS, 2 * SG_HALF), 0.02),
        "sg_ln_g": 1.0 + nrm(ks[17], (N_SG_LAYERS, SG_HALF), 0.02),
        "sg_ln_b": nrm(ks[18], (N_SG_LAYERS, SG_HALF), 0.02),
        "sg_w_s": nrm(ks[19], (N_SG_LAYERS, SG_GROUPS, SG_CHUNK, SG_CHUNK), SG_CHUNK ** -0.5),
        "sg_b_s": 1.0 + nrm(ks[20], (N_SG_LAYERS, SG_GROUPS, SG_CHUNK), 0.02),
        "sg_w_o": nrm(ks[21], (N_SG_LAYERS, SG_HALF, D), SG_HALF ** -0.5),
    }


def reference(x, c, ctx, c_ctx, ada_w, ada_b, norm1_g, norm2_g, mlp_w1, mlp_w2,
              na_w_qkv, na_q_norm, na_k_norm, na_rpb, na_w_o,
              sg_w_in, sg_b_in, sg_ln_g, sg_ln_b, sg_w_s, sg_b_s, sg_w_o):
    last_ctx_layer = ((DEPTH - 1) // N_MIXERS) * N_MIXERS
    silu_c = jax.nn.silu(c)
    silu_cc = jax.nn.silu(c_ctx)
    for i in range(DEPTH):
        ctx_kv = i <= last_ctx_layer
        ctx_full = i < last_ctx_layer
        sh1, sc1, g1, sh2, sc2, g2 = [t[:, None, :] for t in
                                      jnp.split(silu_c @ ada_w[i] + ada_b[i], 6, axis=-1)]
        h = modulate(rms_norm(x, norm1_g[i]), sh1, sc1)
        hc = None
        if ctx_kv:
            csh1, csc1, cg1, csh2, csc2, cg2 = jnp.split(silu_cc @ ada_w[i] + ada_b[i], 6, axis=-1)
            hc = modulate(rms_norm(ctx, norm1_g[i]), csh1, csc1)
        if i % N_MIXERS == 0:
            a = i // N_MIXERS
            y, yc = neighbourhood_attention(h, hc, na_w_qkv[a], na_q_norm[a], na_k_norm[a],
                                            na_rpb[a], na_w_o[a], ctx_full)
        else:
            s = i // N_MIXERS
            y = spatial_gating(h, sg_w_in[s], sg_b_in[s], sg_ln_g[s], sg_ln_b[s],
                               sg_w_s[s], sg_b_s[s], sg_w_o[s])
            yc = None
            if ctx_full:
                yc = spatial_gating(hc, sg_w_in[s], sg_b_in[s], sg_ln_g[s], sg_ln_b[s],
                                    sg_w_s[s], sg_b_s[s], sg_w_o[s])
        x = x + g1 * y
        x = x + g2 * squared_relu_mlp(modulate(rms_norm(x, norm2_g[i]), sh2, sc2), mlp_w1[i], mlp_w2[i])
        if ctx_full:
            ctx = ctx + cg1 * yc
            ctx = ctx + cg2 * squared_relu_mlp(modulate(rms_norm(ctx, norm2_g[i]), csh2, csc2),
                                               mlp_w1[i], mlp_w2[i])
    return x
```

```python
import numpy as np
import concourse.bass as bass
import concourse.mybir as mybir
from concourse.bass_utils import run_bass_kernel_spmd


F32 = mybir.dt.float32
BF16 = mybir.dt.bfloat16
AF = mybir.ActivationFunctionType
ALU = mybir.AluOpType
AX = mybir.AxisListType


class Buf:
    __slots__ = ("name", "w", "r")

    def __init__(self, name=""):
        self.name = name
        self.w = {}
        self.r = {}


class DSem:
    def __init__(self, h):
        self.h = h
        self.count = 0


def _merge(dst, src):
    for k, v in src.items():
        if dst.get(k, -1) < v:
            dst[k] = v


class Prog:
    ENG = ("pe", "act", "dve", "pool", "sp")
    WIN = 3

    def __init__(self, nc):
        self.nc = nc
        self.q = {e: [] for e in self.ENG}
        self.csem = {}
        self.dsems = []
        self.sb_off = 16512
        self.sb_cnt = 0
        self.last_real = {e: -1 for e in self.ENG}

    def sb(self, shape, dtype, name=None):
        esz = 4 if dtype == F32 else 2
        n = 1
        for s in shape[1:]:
            n *= s
        nbytes = (n * esz + 31) // 32 * 32
        off = self.sb_off
        self.sb_off += nbytes
        assert self.sb_off <= 229376, f"sbuf overflow {self.sb_off}"
        self.sb_cnt += 1
        t = self.nc.alloc_sbuf_tensor_at(f"{name or 't'}_{self.sb_cnt}", list(shape), dtype, offset=off)
        return t

    def dsem(self, name):
        h = self.nc.alloc_semaphore(name=name)
        d = DSem(h)
        self.dsems.append(d)
        return d

    def _rec(self, eng, emit, reads, writes, dsem=None, accum=False):
        deps = {}
        for b in reads:
            _merge(deps, b.w)
        for b in writes:
            if not accum:
                _merge(deps, b.w)
            _merge(deps, b.r)
        idx = len(self.q[eng])
        if dsem is None:
            tok = {("c", eng): idx}
        else:
            dsem.count += 16
            tok = {("d", dsem): dsem.count}
        self.q[eng].append([emit, deps, dsem, False, 0])
        if dsem is None:
            self.last_real[eng] = idx
        for b in reads:
            _merge(b.r, tok)
        for b in writes:
            if accum:
                _merge(b.w, tok)
            else:
                b.w = dict(tok)
            b.r = {}

    def op(self, eng, emit, reads=(), writes=()):
        self._rec(eng, emit, reads, writes)

    def dma(self, eng, out, in_, dsem, reads=(), writes=(), accum=False):
        self._rec(eng, lambda e: e.dma_start(out=out, in_=in_), reads, writes, dsem, accum)

    def barrier(self):
        allb = Buf("bar")
        for e in self.ENG:
            if self.last_real[e] >= 0:
                allb.r[("c", e)] = self.last_real[e]
        for d in self.dsems:
            if d.count:
                allb.r[("d", d)] = d.count
        deps = dict(allb.r)
        for e in self.ENG:
            self.q[e].append([None, dict(deps), None, False, 0])

    def _needed(self, E, idx, k, v):
        if k[0] != "c":
            return True
        if k[1] != E:
            return True
        if E == "pe":
            return False
        return idx - v <= self.WIN

    def finalize(self):
        for E in self.ENG:
            for idx, op in enumerate(self.q[E]):
                for k, v in op[1].items():
                    if k[0] == "c" and self._needed(E, idx, k, v):
                        self.q[k[1]][v][3] = True
        for E in self.ENG:
            run = 0
            for op in self.q[E]:
                if op[3]:
                    run += 1
                op[4] = run

    def emit_engine(self, E, e, final_waits=()):
        known = {}
        nw = 0
        for idx, op in enumerate(self.q[E]):
            emit, deps, dsem, sig, sigval = op
            for k, v in deps.items():
                if not self._needed(E, idx, k, v):
                    continue
                if k[0] == "c":
                    sem = self.csem[k[1]]
                    val = self.q[k[1]][v][4]
                    assert self.q[k[1]][v][3]
                else:
                    sem = k[1].h
                    val = v
                key = id(sem)
                if known.get(key, 0) < val:
                    e.wait_ge(sem, val)
                    nw += 1
                    known[key] = val
            if emit is None:
                continue
            ins = emit(e)
            if dsem is not None:
                ins.then_inc(dsem.h, 16)
            elif sig:
                ins.then_inc(self.csem[E], 1)
        for d in final_waits:
            if d.count:
                e.wait_ge(d.h, d.count)
        return nw

    def run_block(self, final_dsems=()):
        nc = self.nc
        self.finalize()
        for E in self.ENG:
            self.csem[E] = nc.alloc_semaphore(name=f"cs_{E}")
        with nc.Block() as block:
            @block.tensor
            def _(e):
                self.emit_engine("pe", e)

            @block.scalar
            def _(e):
                self.emit_engine("act", e)

            @block.vector
            def _(e):
                self.emit_engine("dve", e)

            @block.gpsimd
            def _(e):
                self.emit_engine("pool", e)

            @block.sync
            def _(e):
                self.emit_engine("sp", e, final_waits=final_dsems)


D = 1024
HID = 4096
EPS = 1e-6
NLT = 32
NCT = 2


class K:
    def __init__(self, nlt=NLT, nct=NCT):
        self.nlt = nlt
        self.nct = nct
        self.ntt = nlt + nct
        nc = self.nc = bass.Bass("TRN2", target_bir_lowering=False)
        self.P = Prog(nc)
        P = self.P
        ntok = nlt * 128
        dt = nc.dram_tensor
        self.x_in = dt("x", [ntok, D], F32, kind="ExternalInput").ap()
        self.ctx_in = dt("ctx", [nct * 128, D], F32, kind="ExternalInput").ap()
        self.out = dt("out", [ntok, D], F32, kind="ExternalOutput").ap()
        self.xr = dt("xr", [self.ntt * 128, D], F32).ap()
        self.cvec = dt("cvec", [128, 16], F32, kind="ExternalInput").ap()
        self.ada_w = dt("ada_w", [4, D, 6 * D], F32, kind="ExternalInput").ap()
        self.ada_bc = dt("ada_bc", [128, 4 * 48], F32, kind="ExternalInput").ap()
        self.ada_br = dt("ada_br", [2, 4 * 2 * D], F32, kind="ExternalInput").ap()
        self.ng = dt("ng", [128, 4 * 2 * 8], F32, kind="ExternalInput").ap()
        self.ident_in = dt("ident", [128, 128], F32, kind="ExternalInput").ap()
        self.masks_in = dt("masks", [128, 2], F32, kind="ExternalInput").ap()
        self.w1 = dt("mlp_w1", [4, D, HID], F32, kind="ExternalInput").ap()
        self.w2 = dt("mlp_w2", [4, HID, D], F32, kind="ExternalInput").ap()
        self.na_w_qkv = dt("na_w_qkv", [2, D, 3 * D], F32, kind="ExternalInput").ap()
        self.na_w_o = dt("na_w_o", [2, D, D], F32, kind="ExternalInput").ap()
        self.na_g = dt("na_g", [2, 2 * D], F32, kind="ExternalInput").ap()
        self.rpbG = dt("rpbG", [2, 128, 17 * 1024], F32, kind="ExternalInput").ap()
        self.rpbM = dt("rpbM", [128, 17 * 1024], F32, kind="ExternalInput").ap()
        self.sg_w_in = dt("sg_w_in", [2, D, 6 * D], F32, kind="ExternalInput").ap()
        self.sg_bu = dt("sg_bu", [128, 2 * 24], F32, kind="ExternalInput").ap()
        self.sg_bv = dt("sg_bv", [2, 3 * D], F32, kind="ExternalInput").ap()
        self.sg_lng = dt("sg_lng", [2, 3 * D], F32, kind="ExternalInput").ap()
        self.sg_lnb = dt("sg_lnb", [2, 3 * D], F32, kind="ExternalInput").ap()
        self.sg_wsT = dt("sg_wsT", [2, 128, 24 * 128], F32, kind="ExternalInput").ap()
        self.sg_bs = dt("sg_bs", [2, 24 * 128], F32, kind="ExternalInput").ap()
        self.sg_w_o = dt("sg_w_o", [2, 3 * D, D], F32, kind="ExternalInput").ap()
        self.sT_d = dt("sT_d", [24, 128, self.ntt * 128], BF16).ap()
        self.sT_b = [Buf(f"sT{i}") for i in range(self.ntt)]
        self.gates_d = dt("gates_d", [4 * 2 * 2, D], F32).ap()
        self.xr_b = [Buf(f"xr{t}") for t in range(self.ntt)]
        self.gates_b = Buf("gates")
        self.s_ld = [P.dsem(f"ld{i}") for i in range(4)]
        self.s_st = [P.dsem(f"st{i}") for i in range(2)]
        self.s_w = [P.dsem(f"w{i}") for i in range(4)]
        self.s_misc = P.dsem("misc")
        self.s_x = [P.dsem(f"sx{i}") for i in range(2)]
        self.ps = [nc.alloc_psum_tensor(f"ps{i}", [128, 512], F32) for i in range(7)]
        self.ps_b = [Buf(f"ps{i}") for i in range(7)]
        self.pst = nc.alloc_psum_tensor("pst", [128, 1024], BF16)
        self.pst_b = Buf("pst")
        self.ident = P.sb([128, 128], BF16, "ident")
        self.ident_f = P.sb([128, 128], F32, "identf")
        self.modc = P.sb([128, 4, 48, 2], F32, "modc")
        self.modc_b = Buf("modc")
        self.Acol = P.sb([128, 4, 2, 2, 8], F32, "Acol")
        self.ngc = P.sb([128, 4, 2, 8], F32, "ngc")
        self.const_b = Buf("const")
        self.epsc = P.sb([128, 1], F32, "epsc")
        self.masks = P.sb([128, 2], F32, "masks")
        self.zeroc = P.sb([128, 1], F32, "zeroc")
        self.ones_bf = P.sb([1, 128], BF16, "ones_bf")
        self.persist_end = P.sb_off
        self.dbg_last = False

    def phase_reset(self):
        self.P.sb_off = self.persist_end

    def phase_adaln(self):
        P, nc = self.P, self.nc
        self.phase_reset()
        cv = P.sb([128, 16], F32, "cv")
        s2 = P.sb([128, 8, 2], BF16, "s2")
        adab = P.sb([128, 4 * 48], F32, "adab")
        adabr = P.sb([2, 4 * 2 * D], F32, "adabr")
        wt = [P.sb([128, 8, 512], BF16, f"adaw{i}") for i in range(2)]
        wt_b = [Buf(f"adaw{i}") for i in range(2)]
        rows = P.sb([2, 1024], F32, "rows")
        rows_b = Buf("rows")
        gb = P.sb([128, 512], F32, "gb")
        misc = Buf("misc")
        c = self.const_b
        P.dma("sp", cv[:], self.cvec, self.s_misc, writes=[misc, c], accum=True)
        P.dma("sp", adab[:], self.ada_bc, self.s_misc, writes=[misc, c], accum=True)
        P.dma("sp", adabr[:], self.ada_br, self.s_misc, writes=[misc, c], accum=True)
        P.dma("sp", self.ident_f[:], self.ident_in, self.s_misc, writes=[misc, c], accum=True)
        P.dma("sp", self.masks[:], self.masks_in, self.s_misc, writes=[misc, c], accum=True)
        P.dma("sp", self.ngc[:], self.ng.rearrange("p (l s k) -> p l s k", l=4, s=2), self.s_misc, writes=[misc, c], accum=True)
        P.op("dve", lambda e: e.tensor_copy(out=self.ident[:], in_=self.ident_f[:]), reads=[c], writes=[c])
        P.op("dve", lambda e: e.memset(self.epsc[:], EPS), reads=[], writes=[c])
        P.op("dve", lambda e: e.memset(self.zeroc[:], 0.0), reads=[], writes=[c])
        P.op("dve", lambda e: e.memset(self.ones_bf[:], 1.0), reads=[], writes=[c])
        P.op("act", lambda e: e.activation(out=s2[:].rearrange("p k t -> p t k"),
                                           in_=cv[:].rearrange("p (t k) -> p t k", t=2), func=AF.Silu),
             reads=[misc], writes=[misc])
        psr, psr_b = self.ps[0], self.ps_b[0]
        psc, psc_b = self.ps[1], self.ps_b[1]
        psg, psg_b = self.ps[2], self.ps_b[2]
        nblk = 0
        for l in range(4):
            for j in range(12):
                which = j // 2
                w, wb = wt[nblk % 2], wt_b[nblk % 2]
                sem = self.s_w[nblk % 2]
                nblk += 1
                src = self.ada_w[l].rearrange("(k p) n -> p k n", p=128)[:, :, j * 512:(j + 1) * 512]
                P.dma("pool", w[:], src, sem, writes=[wb])
                if which in (2, 5):
                    gate = 0 if which == 2 else 1
                    half = j % 2
                    for k in range(8):
                        P.op("pe", lambda e, k=k, w=w: e.matmul(psr[0:2, :], lhsT=s2[:, k, :], rhs=w[:, k, :],
                                                                start=(k == 0), stop=(k == 7)),
                             reads=[wb, misc], writes=[psr_b])
                    col0 = (l * 2 + gate) * D + half * 512
                    P.op("dve", lambda e, col0=col0, half=half: e.tensor_tensor(
                        out=rows[:, half * 512:(half + 1) * 512], in0=psr[0:2, :],
                        in1=adabr[:, col0:col0 + 512], op=ALU.add),
                        reads=[psr_b, misc], writes=[rows_b])
                    if half == 1:
                        r0 = (l * 2 + gate) * 2
                        P.dma("sp", self.gates_d[r0:r0 + 2, :], rows[:], self.s_misc, reads=[rows_b],
                              writes=[self.gates_b])
                else:
                    for m in range(4):
                        chunk = j * 4 + m
                        for k in range(8):
                            P.op("pe", lambda e, k=k, w=w, m=m, chunk=chunk: e.matmul(
                                psc[:, chunk * 2:chunk * 2 + 2], lhsT=w[:, k, m * 128:(m + 1) * 128],
                                rhs=s2[:, k, :], start=(k == 0), stop=(k == 7)),
                                reads=[wb, misc], writes=[psc_b])
            for ty in range(2):
                for c0 in (0, 24):
                    P.op("dve", lambda e, l=l, ty=ty, c0=c0: e.tensor_tensor(
                        out=self.modc[:, l, c0:c0 + 16, ty],
                        in0=psc[:, 2 * c0:2 * c0 + 32].rearrange("p (c t) -> p c t", t=2)[:, :, ty],
                        in1=adab[:, l * 48 + c0:l * 48 + c0 + 16], op=ALU.add),
                        reads=[psc_b, misc], writes=[self.modc_b])
            for sub in range(2):
                for ty in range(2):
                    sc0 = (1 + 3 * sub) * 8
                    P.op("dve", lambda e, l=l, sub=sub, ty=ty, sc0=sc0: e.scalar_tensor_tensor(
                        out=self.Acol[:, l, sub, ty, :], in0=self.modc[:, l, sc0:sc0 + 8, ty], scalar=1.0,
                        in1=self.ngc[:, l, sub, :], op0=ALU.add, op1=ALU.mult),
                        reads=[self.modc_b, c], writes=[self.modc_b])
        P.barrier()

    def src_ap(self, t, first):
        if first:
            if t < self.nlt:
                return self.x_in[t * 128:(t + 1) * 128, :]
            return self.ctx_in[(t - self.nlt) * 128:(t - self.nlt + 1) * 128, :]
        return self.xr[t * 128:(t + 1) * 128, :]

    def dst_ap(self, t, last):
        if last and t < self.nlt:
            return self.out[t * 128:(t + 1) * 128, :]
        return self.xr[t * 128:(t + 1) * 128, :]

    def load_gates(self, l, gate, gbc, gbc_b):
        P = self.P
        for ty in range(2):
            r = (l * 2 + gate) * 2 + ty
            P.dma("sp", gbc[:, ty, :], self.gates_d[r:r + 1, :].partition_broadcast(128), self.s_misc,
                  reads=[self.gates_b], writes=[gbc_b])

    def prep_tile(self, t, l, sub, first, st):
        P = self.P
        i = st["n"] % 2
        st["n"] += 1
        xp, xp_b = st["xp"][i], st["xp_b"][i]
        xn, xn_b = st["xn"][i], st["xn_b"][i]
        ss, ss_b = st["ss"][i], st["ss_b"][i]
        P.dma("sp", xp[:], self.src_ap(t, first), self.s_ld[i], reads=[self.xr_b[t]], writes=[xp_b])
        P.op("act", lambda e: e.activation(out=xn[:], in_=xp[:], func=AF.Square, accum_out=ss[:, 0:1]),
             reads=[xp_b], writes=[xn_b, ss_b])
        P.op("act", lambda e: e.activation(out=ss[:, 1:2], in_=ss[:, 0:1], func=AF.Sqrt, scale=1.0 / D, bias=self.epsc[:, 0:1]),
             reads=[ss_b, self.const_b], writes=[ss_b])
        P.op("dve", lambda e: e.reciprocal(out=ss[:, 2:3], in_=ss[:, 1:2]), reads=[ss_b], writes=[ss_b])
        P.op("act", lambda e: e.activation(out=xn[:], in_=xp[:], func=AF.Copy, scale=ss[:, 2:3]),
             reads=[xp_b, ss_b], writes=[xn_b])
        return i

    def transpose_mod(self, i, st, l, sub, ty, hT, hT_b, col0):
        P = self.P
        xn, xn_b = st["xn"][i], st["xn_b"][i]
        pst, pst_b = self.pst, self.pst_b
        for k in range(8):
            P.op("pe", lambda e, k=k: e.transpose(out=pst[:, k * 128:(k + 1) * 128], in_=xn[:, k * 128:(k + 1) * 128],
                                                  identity=self.ident[:]),
                 reads=[xn_b, self.const_b], writes=[pst_b])
        shc = (3 * sub) * 8
        for k in range(8):
            P.op("act", lambda e, k=k: e.activation(
                out=hT[:, k, col0:col0 + 128], in_=pst[:, k * 128:(k + 1) * 128], func=AF.Identity,
                scale=self.Acol[:, l, sub, ty, k:k + 1], bias=self.modc[:, l, shc + k, ty:ty + 1]),
                reads=[pst_b, self.modc_b], writes=[hT_b])

    def phase_mlp(self, l, first=False, last=False, tiles=None):
        P, nc = self.P, self.nc
        self.phase_reset()
        tiles = list(range(self.ntt)) if tiles is None else tiles
        W1 = P.sb([128, 8, HID], BF16, "W1")
        W2 = P.sb([128, 32, D], BF16, "W2")
        W_b = Buf("W")
        st = dict(n=0,
                  xp=[P.sb([128, D], F32, "xp") for _ in range(2)], xp_b=[Buf() for _ in range(2)],
                  xn=[P.sb([128, D], BF16, "xn") for _ in range(2)], xn_b=[Buf() for _ in range(2)],
                  ss=[P.sb([128, 4], F32, "ss") for _ in range(2)], ss_b=[Buf() for _ in range(2)])
        xu = [P.sb([128, D], F32, "xu") for _ in range(2)]
        xu_b = [Buf() for _ in range(2)]
        tmp = [P.sb([128, 512], F32, "tmp") for _ in range(1)]
        tmp_b = [Buf() for _ in range(1)]
        hT = P.sb([128, 8, 512], BF16, "hT")
        hT_b = Buf("hT")
        aT = P.sb([128, 32, 512], BF16, "aT")
        aT_b = Buf("aT")
        sq = [P.sb([128, 512], F32, "sq") for _ in range(2)]
        sq_b = [Buf() for _ in range(2)]
        gbc = P.sb([128, 2, D], F32, "gbc")
        gbc_b = Buf("gbc")
        w1v = self.w1[l].rearrange("(k p) n -> p k n", p=128)
        w2v = self.w2[l].rearrange("(k p) n -> p k n", p=128)
        for k in range(8):
            P.dma("pool", W1[:, k, :], w1v[:, k, :], self.s_w[2], writes=[W_b], accum=True)
        for k4 in range(8):
            P.dma("pool", W2[:, k4 * 4:(k4 + 1) * 4, :], w2v[:, k4 * 4:(k4 + 1) * 4, :], self.s_w[2], writes=[W_b], accum=True)
        self.load_gates(l, 1, gbc, gbc_b)
        blocks = [tiles[i:i + 4] for i in range(0, len(tiles), 4)]
        U = [(self.ps[i], self.ps_b[i]) for i in range(3)]
        Y = [(self.ps[3 + i], self.ps_b[3 + i]) for i in range(4)]
        nu = 0
        nupd = 0

        def prep_norm(blk):
            return [self.prep_tile(t, l, 1, first, st) for t in blk]

        def prep_block(blk):
            for j, t in enumerate(blk):
                i = self.prep_tile(t, l, 1, first, st)
                ty = 0 if t < self.nlt else 1
                self.transpose_mod(i, st, l, 1, ty, hT, hT_b, j * 128)

        prep_block(blocks[0])
        for bi, blk in enumerate(blocks):
            n = len(blk) * 128
            for m in range(32):
                ups, ups_b = U[nu % 3]
                s_, s_b = sq[nu % 2], sq_b[nu % 2]
                nu += 1
                for k in range(8):
                    P.op("pe", lambda e, k=k, m=m, ups=ups, n=n: e.matmul(
                        ups[:, 0:n], lhsT=W1[:, k, m * 128:(m + 1) * 128], rhs=hT[:, k, 0:n],
                        start=(k == 0), stop=(k == 7)), reads=[W_b, hT_b], writes=[ups_b])
                P.op("act", lambda e, ups=ups, s_=s_, n=n: e.activation(out=s_[:, 0:n], in_=ups[:, 0:n], func=AF.Square),
                     reads=[ups_b], writes=[s_b])
                P.op("dve", lambda e, ups=ups, s_=s_, m=m, n=n: e.scalar_tensor_tensor(
                    out=aT[:, m, 0:n], in0=ups[:, 0:n], scalar=0.0, in1=s_[:, 0:n], op0=ALU.is_gt, op1=ALU.mult),
                    reads=[ups_b, s_b], writes=[aT_b])
            if bi + 1 < len(blocks):
                prep_block(blocks[bi + 1])
            for j, t in enumerate(blk):
                ty = 0 if t < self.nlt else 1
                iu = nupd % 2
                nupd += 1
                P.dma("sp", xu[iu][:], self.src_ap(t, first), self.s_ld[2 + iu], reads=[self.xr_b[t]],
                      writes=[xu_b[iu]])
                for h in range(2):
                    yps, yps_b = Y[iu * 2 + h]
                    for kc in range(32):
                        P.op("pe", lambda e, kc=kc, j=j, h=h, yps=yps: e.matmul(
                            yps[:, :], lhsT=aT[:, kc, j * 128:(j + 1) * 128], rhs=W2[:, kc, h * 512:(h + 1) * 512],
                            start=(kc == 0), stop=(kc == 31)), reads=[W_b, aT_b], writes=[yps_b])
                for h in range(2):
                    yps, yps_b = Y[iu * 2 + h]
                    P.op("dve", lambda e, h=h, yps=yps, ty=ty: e.tensor_tensor(
                        out=tmp[0][:, :], in0=yps[:, :], in1=gbc[:, ty, h * 512:(h + 1) * 512],
                        op=ALU.mult), reads=[yps_b, gbc_b], writes=[tmp_b[0]])
                    P.op("pool", lambda e, iu=iu, h=h: e.tensor_tensor(
                        out=xu[iu][:, h * 512:(h + 1) * 512], in0=tmp[0][:, :], in1=xu[iu][:, h * 512:(h + 1) * 512],
                        op=ALU.add), reads=[tmp_b[0], xu_b[iu]], writes=[xu_b[iu]])
                P.dma("sp", self.dst_ap(t, last), xu[iu][:], self.s_st[iu], reads=[xu_b[iu]], writes=[self.xr_b[t]])
        P.barrier()


    @staticmethod
    def half_index(r, kt):
        rs = min(max(r - 4, 0), 56)
        v0 = rs <= 2 * kt < rs + 8
        v1 = rs <= 2 * kt + 1 < rs + 8
        rr0 = 2 * kt - r + 7
        if v0 and v1:
            assert 0 <= rr0 <= 13
            return rr0
        if v1:
            assert rr0 == 2
            return 14
        if v0:
            assert rr0 == 10
            return 15
        return 16

    @staticmethod
    def klist(t):
        r0, r1 = 2 * t, 2 * t + 1
        rs0 = min(max(r0 - 4, 0), 56)
        rs1 = min(max(r1 - 4, 0), 56)
        return list(range(rs0 // 2, (rs1 + 7) // 2 + 1))

    def phase_attn(self, l, first=False, lat_tiles=None):
        P, nc = self.P, self.nc
        self.phase_reset()
        a = l // 2
        ctx_q = (l == 0)
        nlt = self.nlt
        Wqkv = P.sb([128, 8, 3 * D], BF16, "Wqkv")
        Wo = P.sb([128, 8, D], BF16, "Wo")
        tab = P.sb([128, 17, 16, 64], BF16, "tab")
        gqk = P.sb([128, 2, D], F32, "gqk")
        W_b = Buf("W")
        tab_b = Buf("tab")
        wq = self.na_w_qkv[a].rearrange("(k p) n -> p k n", p=128)
        for k in range(8):
            P.dma("pool", Wqkv[:, k, :], wq[:, k, :], self.s_w[2], writes=[W_b], accum=True)
        P.dma("pool", Wo[:], self.na_w_o[a].rearrange("(k p) n -> p k n", p=128), self.s_w[2], writes=[W_b], accum=True)
        P.dma("sp", gqk[:].rearrange("p a d -> p (a d)"), self.na_g[a:a + 1, :].partition_broadcast(128), self.s_w[3],
              writes=[W_b], accum=True)
        mark = P.sb_off
        stg = [P.sb([128, 1024], F32, "stg") for _ in range(2)]
        stm = [P.sb([128, 1024], F32, "stm") for _ in range(2)]
        stg_b = [Buf() for _ in range(2)]
        stm_b = [Buf() for _ in range(2)]
        for i in range(17):
            j = i % 2
            P.dma("sp", stg[j][:], self.rpbG[a][:, i * 1024:(i + 1) * 1024], self.s_ld[j], writes=[stg_b[j]])
            P.dma("sp", stm[j][:], self.rpbM[:, i * 1024:(i + 1) * 1024], self.s_ld[2 + j], writes=[stm_b[j]])
            P.op("dve", lambda e, i=i, j=j: e.scalar_tensor_tensor(
                out=tab[:, i, :, :].rearrange("p h c -> p (h c)"), in0=stg[j][:], scalar=8.0, in1=stm[j][:],
                op0=ALU.mult, op1=ALU.add), reads=[stg_b[j], stm_b[j]], writes=[tab_b])
        P.barrier()
        P.sb_off = mark
        NK, NQ = 6, 4
        kT = [P.sb([128, 8, 128], BF16, "kT") for _ in range(NK + 2)]
        kT_b = [Buf() for _ in range(NK + 2)]
        qT = [[P.sb([128, 8, 128], BF16, "qT") for _ in range(2)] for _ in range(NQ)]
        qT_b = [Buf() for _ in range(NQ)]
        for i in range(NQ):
            pass
        Va = [P.sb([128, 16, 65], BF16, "Va") for _ in range(NK + 2)]
        Va_b = [Buf() for _ in range(NK + 2)]
        st = dict(n=0,
                  xp=[P.sb([128, D], F32, "xp") for _ in range(2)], xp_b=[Buf() for _ in range(2)],
                  xn=[P.sb([128, D], BF16, "xn") for _ in range(2)], xn_b=[Buf() for _ in range(2)],
                  ss=[P.sb([128, 4], F32, "ss") for _ in range(2)], ss_b=[Buf() for _ in range(2)])
        hT1 = P.sb([128, 8, 128], BF16, "hT1")
        hT1_b = Buf()
        raw = P.sb([128, D], F32, "raw")
        raw_b = Buf()
        sqt = [P.sb([128, 512], F32, "sqt") for _ in range(2)]
        sqt_b = [Buf() for _ in range(2)]
        msb = P.sb([128, 96], F32, "msb")
        msb_b = Buf()
        qkn = P.sb([128, D], BF16, "qkn")
        qkn_b = Buf()
        PT = [P.sb([128, 7, 128], BF16, "PT") for _ in range(3)]
        PT_b = [Buf() for _ in range(3)]
        rden = P.sb([128, 16], F32, "rden")
        rden_b = Buf()
        on = P.sb([128, 16, 64], BF16, "on")
        on_b = Buf()
        oT = P.sb([128, 8, 128], BF16, "oT")
        oT_b = Buf()
        xu = [P.sb([128, D], F32, "xu") for _ in range(1)]
        xu_b = [Buf() for _ in range(1)]
        tmp = P.sb([128, 512], F32, "tmp")
        tmp_b = Buf()
        gbc = P.sb([128, 2, D], F32, "gbc")
        gbc_b = Buf("gbc")
        self.load_gates(l, 0, gbc, gbc_b)
        G = [(self.ps[0], self.ps_b[0]), (self.ps[1], self.ps_b[1])]
        S = [(self.ps[2 + i], self.ps_b[2 + i]) for i in range(3)]
        O = [(self.ps[5], self.ps_b[5]), (self.ps[6], self.ps_b[6]), G[0]]
        Yb = [G[1], S[0]]
        pst, pst_b = self.pst, self.pst_b
        cnt = dict(g=0, s=0, pt=0, sq=0)
        for i in range(NK + 2):
            P.op("dve", lambda e, i=i: e.memset(Va[i][:].rearrange("p h c -> p (h c)"), 1.0), writes=[Va_b[i]])

        def qkv(t, kslot, qslot):
            ty = 0 if t < nlt else 1
            i = self.prep_tile(t, l, 0, first, st)
            self.transpose_mod(i, st, l, 0, ty, hT1, hT1_b, 0)
            for part in range(2):
                if part == 0 and qslot is None:
                    continue
                for half in range(2):
                    c6 = part * 2 + half
                    gps, gps_b = G[cnt["g"] % 2]
                    cnt["g"] += 1
                    sq_, sq_b = sqt[cnt["sq"] % 2], sqt_b[cnt["sq"] % 2]
                    cnt["sq"] += 1
                    for k in range(8):
                        P.op("pe", lambda e, k=k, c6=c6, gps=gps: e.matmul(
                            gps[:, :], lhsT=hT1[:, k, :], rhs=Wqkv[:, k, c6 * 512:(c6 + 1) * 512],
                            start=(k == 0), stop=(k == 7)), reads=[W_b, hT1_b], writes=[gps_b])
                    P.op("act", lambda e, half=half, gps=gps: e.activation(out=raw[:, half * 512:(half + 1) * 512],
                                                                           in_=gps[:, :], func=AF.Copy), reads=[gps_b], writes=[raw_b])
                    P.op("act", lambda e, gps=gps, sq_=sq_: e.activation(out=sq_[:], in_=gps[:, :], func=AF.Square),
                         reads=[gps_b], writes=[sq_b])
                    if "nored" not in getattr(self, "dbg_mode", ""):
                      P.op("dve", lambda e, half=half, sq_=sq_, part=part: e.tensor_reduce(
                        out=msb[:, part * 16 + half * 8:part * 16 + half * 8 + 8],
                        in_=sq_[:].rearrange("p (h d) -> p h d", d=64), axis=AX.X, op=ALU.add),
                        reads=[sq_b], writes=[msb_b])
                o0 = part * 16
                P.op("act", lambda e, o0=o0: e.activation(out=msb[:, 32 + o0:48 + o0], in_=msb[:, o0:o0 + 16], func=AF.Sqrt,
                                                          scale=1.0 / 64, bias=self.epsc[:, 0:1]),
                     reads=[msb_b, self.const_b], writes=[msb_b])
                P.op("dve", lambda e, o0=o0: e.reciprocal(out=msb[:, 64 + o0:80 + o0], in_=msb[:, 32 + o0:48 + o0]),
                     reads=[msb_b], writes=[msb_b])
                for hh_ in range(16):
                    P.op("dve", lambda e, o0=o0, hh_=hh_: e.tensor_scalar(
                        out=raw[:, hh_ * 64:(hh_ + 1) * 64], in0=raw[:, hh_ * 64:(hh_ + 1) * 64],
                        scalar1=1.0, scalar2=msb[:, 64 + o0 + hh_:65 + o0 + hh_], op0=ALU.mult, op1=ALU.mult),
                        reads=[raw_b, msb_b], writes=[raw_b])
                P.op("pool", lambda e, part=part: e.tensor_tensor(out=qkn[:], in0=raw[:], in1=gqk[:, part, :], op=ALU.mult),
                     reads=[raw_b, W_b], writes=[qkn_b])
                for k in range(8):
                    P.op("pe", lambda e, k=k: e.transpose(out=pst[:, k * 128:(k + 1) * 128],
                                                          in_=qkn[:, k * 128:(k + 1) * 128], identity=self.ident[:]),
                         reads=[qkn_b, self.const_b], writes=[pst_b])
                if part == 0:
                    for hp_ in range(2):
                        P.op("act", lambda e, hp_=hp_: e.activation(
                            out=qT[qslot][hp_][:].rearrange("p k t -> p (k t)"), in_=pst[:, :], func=AF.Identity,
                            scale=self.masks[:, hp_:hp_ + 1], bias=self.zeroc[:, 0:1]),
                            reads=[pst_b, self.const_b], writes=[qT_b[qslot]])
                else:
                    dst, dst_b = kT[kslot], kT_b[kslot]
                    P.op("act", lambda e, dst=dst: e.activation(out=dst[:].rearrange("p k t -> p (k t)"), in_=pst[:, :],
                                                                func=AF.Copy), reads=[pst_b], writes=[dst_b])
            for half in range(2):
                c6 = 4 + half
                gps, gps_b = G[cnt["g"] % 2]
                cnt["g"] += 1
                for k in range(8):
                    P.op("pe", lambda e, k=k, c6=c6, gps=gps: e.matmul(
                        gps[:, :], lhsT=hT1[:, k, :], rhs=Wqkv[:, k, c6 * 512:(c6 + 1) * 512],
                        start=(k == 0), stop=(k == 7)), reads=[W_b, hT1_b], writes=[gps_b])
                for h8 in range(8):
                    P.op("act", lambda e, half=half, gps=gps, h8=h8: e.activation(
                        out=Va[kslot][:, half * 8 + h8, 0:64], in_=gps[:, h8 * 64:(h8 + 1) * 64],
                        func=AF.Copy), reads=[gps_b], writes=[Va_b[kslot]])

        def attn(t, keys, qslot):
            if getattr(self, "dbg_skip_attn", False):
                return
            ty = 0 if t < nlt else 1
            P.dma("sp", xu[0][:], self.src_ap(t, first), self.s_ld[2], reads=[self.xr_b[t]], writes=[xu_b[0]])
            nreg = len(keys)
            pend = None

            def pv(h, pi):
                ob, ob_b = O[h // 6]
                oc = (h % 6) * 65
                for ri, key in enumerate(keys):
                    ks = key[2]
                    P.op("pe", lambda e, ri=ri, ks=ks, h=h, pi=pi, ob=ob, oc=oc: e.matmul(
                        ob[:, oc:oc + 65], lhsT=PT[pi][:, ri, :], rhs=Va[ks][:, h, :],
                        start=(ri == 0), stop=(ri == nreg - 1)), reads=[PT_b[pi], Va_b[ks]], writes=[ob_b])

            for h in range(16):
                hp, hc = h % 2, h // 2
                p0 = hp * 64
                pi = cnt["pt"] % 3
                cnt["pt"] += 1
                nsl = (nreg + 3) // 4
                slots = []
                for _ in range(nsl):
                    slots.append(S[cnt["s"] % 3])
                    cnt["s"] += 1
                for ri, key in enumerate(keys):
                    sps, sps_b = slots[ri // 4]
                    col = (ri % 4) * 128
                    ks = key[2]
                    if key[0] == "l":
                        kt = key[1]
                        i0 = self.half_index(2 * t, kt)
                        i1 = self.half_index(2 * t + 1, kt)
                        P.op("pe", lambda e, sps=sps, col=col, i0=i0, h=h: e.matmul(
                            sps[:, col:col + 64], lhsT=self.ident[:], rhs=tab[:, i0, h, :], start=True, stop=False),
                            reads=[tab_b, self.const_b], writes=[sps_b])
                        P.op("pe", lambda e, sps=sps, col=col, i1=i1, h=h: e.matmul(
                            sps[:, col + 64:col + 128], lhsT=self.ident[:], rhs=tab[:, i1, h, :], start=False, stop=False),
                            reads=[tab_b, self.const_b], writes=[sps_b])
                        P.op("pe", lambda e, sps=sps, col=col, ks=ks, hp=hp, hc=hc: e.matmul(
                            sps[:, col:col + 128], lhsT=kT[ks][:, hc, :], rhs=qT[qslot][hp][:, hc, :],
                            start=False, stop=True), reads=[kT_b[ks], qT_b[qslot]], writes=[sps_b])
                    else:
                        P.op("pe", lambda e, sps=sps, col=col, ks=ks, hp=hp, hc=hc: e.matmul(
                            sps[:, col:col + 128], lhsT=kT[ks][:, hc, :], rhs=qT[qslot][hp][:, hc, :],
                            start=True, stop=True), reads=[kT_b[ks], qT_b[qslot]], writes=[sps_b])
                for si_, (sps, sps_b) in enumerate(slots):
                    r0 = si_ * 4
                    nr = min(4, nreg - r0)
                    P.op("act", lambda e, sps=sps, r0=r0, nr=nr, pi=pi: e.activation(
                        out=PT[pi][:, r0:r0 + nr, :].rearrange("p r q -> p (r q)"), in_=sps[:, 0:nr * 128],
                        func=AF.Exp, scale=0.125), reads=[sps_b], writes=[PT_b[pi]])
                if pend is not None:
                    pv(*pend)
                pend = (h, pi)
            pv(*pend)
            for ob_i, (h0, h1) in enumerate([(0, 6), (6, 12), (12, 16)]):
                ob, ob_b = O[ob_i]
                nh = h1 - h0
                ov = ob[:, 0:nh * 65].rearrange("p (h c) -> p h c", c=65)
                P.op("dve", lambda e, ov=ov, h0=h0, h1=h1: e.reciprocal(out=rden[:, h0:h1], in_=ov[:, :, 64]),
                     reads=[ob_b], writes=[rden_b])
                for hh_ in range(h0, h1):
                    P.op("dve", lambda e, ov=ov, hh_=hh_, h0=h0: e.tensor_scalar(
                        out=on[:, hh_, :], in0=ov[:, hh_ - h0, 0:64], scalar1=1.0, scalar2=rden[:, hh_:hh_ + 1],
                        op0=ALU.mult, op1=ALU.mult), reads=[ob_b, rden_b], writes=[on_b])
            for k in range(8):
                P.op("pe", lambda e, k=k: e.transpose(
                    out=pst[:, k * 128:(k + 1) * 128],
                    in_=on[:].rearrange("p h d -> p (h d)")[:, k * 128:(k + 1) * 128], identity=self.ident[:]),
                    reads=[on_b, self.const_b], writes=[pst_b])
            P.op("act", lambda e: e.activation(out=oT[:].rearrange("p k t -> p (k t)"), in_=pst[:, :], func=AF.Copy),
                 reads=[pst_b], writes=[oT_b])
            for hh in range(2):
                yps, yps_b = Yb[hh]
                for k in range(8):
                    P.op("pe", lambda e, k=k, hh=hh, yps=yps: e.matmul(
                        yps[:, :], lhsT=oT[:, k, :], rhs=Wo[:, k, hh * 512:(hh + 1) * 512],
                        start=(k == 0), stop=(k == 7)), reads=[W_b, oT_b], writes=[yps_b])
            self.update_tile(t, 0, ty, Yb, xu, xu_b, tmp, tmp_b, gbc, gbc_b, last=self.dbg_last)

        if getattr(self, "dbg_mode", "") == "tab":
            P.barrier()
            return
        cslot = [NK, NK + 1]
        for ci in range(self.nct):
            qkv(nlt + ci, cslot[ci], (ci % NQ) if ctx_q else None)
        ckeys = [("c", ci, cslot[ci]) for ci in range(self.nct)]
        if ctx_q:
            for ci in range(self.nct):
                attn(nlt + ci, ckeys, ci % NQ)
        lat = list(range(nlt)) if lat_tiles is None else lat_tiles
        done = 0
        for t in lat:
            kl = [kt for kt in self.klist(t)]
            kmax = max(kl)
            while done <= kmax:
                qkv(done, done % NK, done % NQ)
                done += 1
            keys = [("l", kt, kt % NK) for kt in kl] + ckeys
            attn(t, keys, t % NQ)
        P.barrier()

    def phase_sg_a(self, l, first=False, tiles=None):
        P, nc = self.P, self.nc
        self.phase_reset()
        si = l // 2
        tiles = list(range(self.ntt)) if tiles is None else tiles
        Wv = P.sb([128, 8, 3 * D], BF16, "Wv")
        WsT = P.sb([128, 24, 128], BF16, "WsT")
        Gbc = P.sb([128, 3 * D], F32, "Gbc")
        Bbc = P.sb([128, 3 * D], F32, "Bbc")
        bv = P.sb([1, 3 * D], BF16, "bv")
        bsbc = P.sb([128, 24, 128], F32, "bsbc")
        W_b = Buf("W")
        st = dict(n=0,
                  xp=[P.sb([128, D], F32, "xp") for _ in range(2)], xp_b=[Buf() for _ in range(2)],
                  xn=[P.sb([128, D], BF16, "xn") for _ in range(2)], xn_b=[Buf() for _ in range(2)],
                  ss=[P.sb([128, 4], F32, "ss") for _ in range(2)], ss_b=[Buf() for _ in range(2)])
        hT = P.sb([128, 8, 512], BF16, "hT")
        hT_b = Buf("hT")
        v = [P.sb([128, 3 * D], F32, "v") for _ in range(2)]
        v_b = [Buf() for _ in range(2)]
        vn = P.sb([128, 4, 3 * D], BF16, "vn")
        vn_b = [Buf() for _ in range(4)]
        stage = P.sb([128, 24, 512], BF16, "stage")
        stage_b = Buf("stage")
        lst = [P.sb([128, 48], F32, "lst") for _ in range(2)]
        lst_b = [Buf() for _ in range(2)]
        wv = self.sg_w_in[si].rearrange("(k p) n -> p k n", p=128)
        for k in range(8):
            P.dma("pool", Wv[:, k, :], wv[:, k, 3 * D:6 * D], self.s_w[2], writes=[W_b], accum=True)
        P.dma("pool", WsT[:].rearrange("p g q -> p (g q)"), self.sg_wsT[si], self.s_w[2], writes=[W_b], accum=True)
        P.dma("pool", bv[:], self.sg_bv[si:si + 1, :], self.s_w[2], writes=[W_b], accum=True)
        P.dma("sp", Gbc[:], self.sg_lng[si:si + 1, :].partition_broadcast(128), self.s_w[3], writes=[W_b], accum=True)
        P.dma("sp", Bbc[:], self.sg_lnb[si:si + 1, :].partition_broadcast(128), self.s_w[3], writes=[W_b], accum=True)
        P.dma("sp", bsbc[:].rearrange("p g q -> p (g q)"), self.sg_bs[si:si + 1, :].partition_broadcast(128),
              self.s_w[3], writes=[W_b], accum=True)
        blocks = [tiles[i:i + 4] for i in range(0, len(tiles), 4)]
        VP = [(self.ps[i], self.ps_b[i]) for i in range(4)]
        SP_ = [(self.ps[4 + i], self.ps_b[4 + i]) for i in range(3)]
        nv = 0
        nsp = 0
        nt = 0

        def prep_block(blk):
            for j, t in enumerate(blk):
                i = self.prep_tile(t, l, 0, first, st)
                ty = 0 if t < self.nlt else 1
                self.transpose_mod(i, st, l, 0, ty, hT, hT_b, j * 128)

        prep_block(blocks[0])
        for bi, blk in enumerate(blocks):
            nb = len(blk)
            n = nb * 128
            for j, t in enumerate(blk):
                vi = nt % 2
                nt += 1
                vt, vt_b = v[vi], v_b[vi]
                ls, ls_b = lst[vi], lst_b[vi]
                for c6 in range(6):
                    vps, vps_b = VP[nv % 4]
                    nv += 1
                    P.op("pe", lambda e, vps=vps, c6=c6: e.matmul(vps[:, :], lhsT=self.ones_bf[0:1, :],
                                                                   rhs=bv[0:1, c6 * 512:(c6 + 1) * 512], start=True, stop=False),
                         reads=[W_b, self.const_b], writes=[vps_b])
                    for k in range(8):
                        P.op("pe", lambda e, vps=vps, c6=c6, k=k, j=j: e.matmul(
                            vps[:, :], lhsT=hT[:, k, j * 128:(j + 1) * 128], rhs=Wv[:, k, c6 * 512:(c6 + 1) * 512],
                            start=False, stop=(k == 7)), reads=[W_b, hT_b], writes=[vps_b])
                    P.op("act", lambda e, vps=vps, c6=c6, vt=vt: e.activation(
                        out=vt[:, c6 * 512:(c6 + 1) * 512], in_=vps[:, :], func=AF.Gelu), reads=[vps_b], writes=[vt_b])
                    P.op("dve", lambda e, c6=c6, vt=vt, ls=ls: e.bn_stats(out=ls[:, c6 * 6:(c6 + 1) * 6],
                                                                          in_=vt[:, c6 * 512:(c6 + 1) * 512]),
                         reads=[vt_b], writes=[ls_b])
                P.op("dve", lambda e, ls=ls: e.bn_aggr(out=ls[:, 36:38], in_=ls[:, 0:36]), reads=[ls_b], writes=[ls_b])
                P.op("act", lambda e, ls=ls: e.activation(out=ls[:, 38:39], in_=ls[:, 37:38], func=AF.Sqrt,
                                                          bias=self.epsc[:, 0:1]), reads=[ls_b, self.const_b], writes=[ls_b])
                P.op("dve", lambda e, ls=ls: e.reciprocal(out=ls[:, 39:40], in_=ls[:, 38:39]), reads=[ls_b], writes=[ls_b])
                P.op("dve", lambda e, ls=ls: e.tensor_scalar(out=ls[:, 40:41], in0=ls[:, 36:37], scalar1=-1.0,
                                                             scalar2=ls[:, 39:40], op0=ALU.mult, op1=ALU.mult),
                     reads=[ls_b], writes=[ls_b])
                P.op("act", lambda e, ls=ls, vt=vt: e.activation(out=vt[:], in_=vt[:], func=AF.Identity,
                                                                 scale=ls[:, 39:40], bias=ls[:, 40:41]),
                     reads=[ls_b, vt_b], writes=[vt_b])
                P.op("dve", lambda e, vt=vt: e.tensor_tensor(out=vt[:], in0=vt[:], in1=Gbc[:], op=ALU.mult),
                     reads=[vt_b, W_b], writes=[vt_b])
                P.op("pool", lambda e, vt=vt, j=j: e.tensor_tensor(out=vn[:, j, :], in0=vt[:], in1=Bbc[:], op=ALU.add),
                     reads=[vt_b, W_b], writes=[vn_b[j]])
            if bi + 1 < len(blocks):
                prep_block(blocks[bi + 1])
            for g in range(24):
                sps, sps_b = SP_[nsp % 3]
                nsp += 1
                for j in range(nb):
                    P.op("pe", lambda e, sps=sps, g=g, j=j: e.matmul(
                        sps[:, j * 128:(j + 1) * 128], lhsT=vn[:, j, g * 128:(g + 1) * 128], rhs=WsT[:, g, :],
                        start=True, stop=True), reads=[W_b, vn_b[j]], writes=[sps_b])
                for j in range(nb):
                    P.op("dve", lambda e, sps=sps, g=g, j=j: e.tensor_tensor(
                        out=stage[:, g, j * 128:(j + 1) * 128], in0=sps[:, j * 128:(j + 1) * 128], in1=bsbc[:, g, :],
                        op=ALU.add), reads=[sps_b, W_b], writes=[stage_b])
            t0 = blk[0]
            P.dma("sp", self.sT_d.rearrange("g c t -> c g t")[:, :, t0 * 128:t0 * 128 + n], stage[:, :, 0:n],
                  self.s_st[0], reads=[stage_b], writes=[self.sT_b[t] for t in blk])
        P.barrier()

    def phase_sg_b(self, l, first=False, tiles=None):
        P, nc = self.P, self.nc
        self.phase_reset()
        si = l // 2
        tiles = list(range(self.ntt)) if tiles is None else tiles
        Wu = P.sb([128, 8, 3 * D], BF16, "Wu")
        Wo = P.sb([128, 24, D], BF16, "Wo")
        bu = P.sb([128, 48], F32, "bu")
        W_b = Buf("W")
        st = dict(n=0,
                  xp=[P.sb([128, D], F32, "xp") for _ in range(2)], xp_b=[Buf() for _ in range(2)],
                  xn=[P.sb([128, D], BF16, "xn") for _ in range(2)], xn_b=[Buf() for _ in range(2)],
                  ss=[P.sb([128, 4], F32, "ss") for _ in range(2)], ss_b=[Buf() for _ in range(2)])
        hT = P.sb([128, 8, 512], BF16, "hT")
        hT_b = Buf("hT")
        sT = [P.sb([128, 24, 512], BF16, "sT") for _ in range(2)]
        sT_b = [Buf() for _ in range(2)]
        ut = [P.sb([128, 512], BF16, "ut") for _ in range(2)]
        ut_b = [Buf() for _ in range(2)]
        xu = [P.sb([128, D], F32, "xu") for _ in range(2)]
        xu_b = [Buf() for _ in range(2)]
        tmp = P.sb([128, 512], F32, "tmp")
        tmp_b = Buf()
        gbc = P.sb([128, 2, D], F32, "gbc")
        gbc_b = Buf("gbc")
        wv = self.sg_w_in[si].rearrange("(k p) n -> p k n", p=128)
        for k in range(8):
            P.dma("pool", Wu[:, k, :], wv[:, k, 0:3 * D], self.s_w[2], writes=[W_b], accum=True)
        wo = self.sg_w_o[si].rearrange("(g p) n -> p g n", p=128)
        for g4 in range(6):
            P.dma("pool", Wo[:, g4 * 4:(g4 + 1) * 4, :], wo[:, g4 * 4:(g4 + 1) * 4, :], self.s_w[2], writes=[W_b], accum=True)
        P.dma("sp", bu[:], self.sg_bu, self.s_w[3], writes=[W_b], accum=True)
        self.load_gates(l, 0, gbc, gbc_b)
        blocks = [tiles[i:i + 4] for i in range(0, len(tiles), 4)]
        U = [(self.ps[i], self.ps_b[i]) for i in range(3)]
        Y = [(self.ps[3 + i], self.ps_b[3 + i]) for i in range(4)]
        nu = 0
        nupd = 0

        def prep_block(blk, bi):
            n = len(blk) * 128
            t0 = blk[0]
            P.dma("sp", sT[bi % 2][:, :, 0:n], self.sT_d.rearrange("g c t -> c g t")[:, :, t0 * 128:t0 * 128 + n],
                  self.s_x[bi % 2], reads=[self.sT_b[t] for t in blk], writes=[sT_b[bi % 2]])
            for j, t in enumerate(blk):
                i = self.prep_tile(t, l, 0, first, st)
                ty = 0 if t < self.nlt else 1
                self.transpose_mod(i, st, l, 0, ty, hT, hT_b, j * 128)

        prep_block(blocks[0], 0)
        for bi, blk in enumerate(blocks):
            nb = len(blk)
            n = nb * 128
            sTb, sTb_b = sT[bi % 2], sT_b[bi % 2]
            for g in range(24):
                ups, ups_b = U[nu % 3]
                u_, u_b = ut[nu % 2], ut_b[nu % 2]
                nu += 1
                for k in range(8):
                    P.op("pe", lambda e, k=k, g=g, ups=ups, n=n: e.matmul(
                        ups[:, 0:n], lhsT=Wu[:, k, g * 128:(g + 1) * 128], rhs=hT[:, k, 0:n],
                        start=(k == 0), stop=(k == 7)), reads=[W_b, hT_b], writes=[ups_b])
                P.op("act", lambda e, ups=ups, u_=u_, n=n, g=g: e.activation(
                    out=u_[:, 0:n], in_=ups[:, 0:n], func=AF.Gelu, bias=bu[:, si * 24 + g:si * 24 + g + 1]),
                    reads=[ups_b, W_b], writes=[u_b])
                P.op("dve", lambda e, u_=u_, g=g, n=n, sTb=sTb: e.tensor_tensor(
                    out=sTb[:, g, 0:n], in0=sTb[:, g, 0:n], in1=u_[:, 0:n], op=ALU.mult),
                    reads=[u_b, sTb_b], writes=[sTb_b])
            if bi + 1 < len(blocks):
                prep_block(blocks[bi + 1], bi + 1)
            for j, t in enumerate(blk):
                ty = 0 if t < self.nlt else 1
                iu = nupd % 2
                nupd += 1
                P.dma("sp", xu[iu][:], self.src_ap(t, first), self.s_ld[2 + iu], reads=[self.xr_b[t]],
                      writes=[xu_b[iu]])
                for h in range(2):
                    yps, yps_b = Y[iu * 2 + h]
                    for g in range(24):
                        P.op("pe", lambda e, g=g, j=j, h=h, yps=yps, sTb=sTb: e.matmul(
                            yps[:, :], lhsT=sTb[:, g, j * 128:(j + 1) * 128], rhs=Wo[:, g, h * 512:(h + 1) * 512],
                            start=(g == 0), stop=(g == 23)), reads=[W_b, sTb_b], writes=[yps_b])
                self.update_tile(t, iu, ty, Y, xu, xu_b, tmp, tmp_b, gbc, gbc_b, last=self.dbg_last)
        P.barrier()

    def update_tile(self, t, iu, ty, Y, xu, xu_b, tmp, tmp_b, gbc, gbc_b, last):
        P = self.P
        for h in range(2):
            yps, yps_b = Y[iu * 2 + h]
            P.op("dve", lambda e, h=h, yps=yps: e.tensor_tensor(
                out=tmp[:, :], in0=yps[:, :], in1=gbc[:, ty, h * 512:(h + 1) * 512],
                op=ALU.mult), reads=[yps_b, gbc_b], writes=[tmp_b])
            P.op("pool", lambda e, h=h: e.tensor_tensor(
                out=xu[iu][:, h * 512:(h + 1) * 512], in0=tmp[:, :], in1=xu[iu][:, h * 512:(h + 1) * 512],
                op=ALU.add), reads=[tmp_b, xu_b[iu]], writes=[xu_b[iu]])
        P.dma("sp", self.dst_ap(t, last), xu[iu][:], self.s_st[iu], reads=[xu_b[iu]], writes=[self.xr_b[t]])

    def finish(self):
        self.P.run_block(final_dsems=self.s_st + [self.s_misc])
        return self.nc


_RPB_CACHE = {}


def rpb_tables(rpb):
    kr = np.arange(2)[:, None, None]
    kc = np.arange(64)[None, :, None]
    qc = np.arange(64)[None, None, :]
    cr = np.clip(kc - qc + 15, 0, 30)
    wstart = np.clip(qc - 8, 0, 48)
    cvalid = (kc >= wstart) & (kc < wstart + 16)
    G = np.zeros((2, 2, 64, 17, 16, 64), np.float32)
    M = np.zeros((2, 64, 17, 16, 64), np.float32)
    for i in range(17):
        if i < 14:
            rr0, rv = i, (True, True)
        elif i == 14:
            rr0, rv = 2, (False, True)
        elif i == 15:
            rr0, rv = 10, (True, False)
        else:
            rr0, rv = 0, (False, False)
        rr = np.clip(rr0 + kr, 0, 14)
        rrb = np.broadcast_to(rr, (2, 64, 64))
        crb = np.broadcast_to(cr, (2, 64, 64))
        g = rpb[:, :, rrb, crb]
        G[:, :, :, i, :, :] = g.transpose(0, 2, 3, 1, 4)
        valid = np.broadcast_to(cvalid, (2, 64, 64)) & np.array(rv)[:, None, None]
        M[:, :, i, :, :] = np.where(valid, 0.0, -240000.0)[:, :, None, :]
    return (np.ascontiguousarray(G.reshape(2, 128, 17 * 1024)), np.ascontiguousarray(M.reshape(128, 17 * 1024)))


def host_core_inputs(inp, b, nlt=NLT, nct=NCT):
    f = np.float32
    c = np.asarray(inp["c"][b], f)
    cc = np.asarray(inp["c_ctx"], f)
    col = lambda v: np.ascontiguousarray(v.reshape(-1, 128).T)
    m = {}
    m["x"] = np.ascontiguousarray(inp["x"][b][:nlt * 128])
    m["ctx"] = np.ascontiguousarray(inp["ctx"][b][:nct * 128])
    m["cvec"] = np.ascontiguousarray(np.concatenate([col(c), col(cc)], axis=1))
    return m


def host_inputs(inp, b, nlt=NLT, nct=NCT):
    f = np.float32
    col = lambda v: np.ascontiguousarray(v.reshape(-1, 128).T)
    m = host_core_inputs(inp, b, nlt, nct)
    m["ada_w"] = np.asarray(inp["ada_w"], f)
    ab = np.asarray(inp["ada_b"], f)
    m["ada_bc"] = np.ascontiguousarray(np.concatenate([col(ab[l]) for l in range(4)], axis=1))
    gates = np.stack([np.stack([ab[l, 2 * D:3 * D], ab[l, 5 * D:6 * D]]) for l in range(4)])
    m["ada_br"] = np.ascontiguousarray(np.broadcast_to(gates.reshape(1, -1), (2, 4 * 2 * D)))
    n1 = np.asarray(inp["norm1_g"], f)
    n2 = np.asarray(inp["norm2_g"], f)
    m["ng"] = np.ascontiguousarray(np.concatenate(
        [np.concatenate([col(n1[l]), col(n2[l])], axis=1) for l in range(4)], axis=1))
    m["ident"] = np.eye(128, dtype=f)
    mk = np.zeros((128, 2), f)
    mk[:64, 0] = 1
    mk[64:, 1] = 1
    m["masks"] = mk
    m["na_w_qkv"] = np.asarray(inp["na_w_qkv"], f)
    m["na_w_o"] = np.asarray(inp["na_w_o"], f)
    qn = np.asarray(inp["na_q_norm"], f)
    kn = np.asarray(inp["na_k_norm"], f)
    m["na_g"] = np.ascontiguousarray(np.stack([np.concatenate([np.tile(qn[a_], 16), np.tile(kn[a_], 16)]) for a_ in range(2)]))
    G_, M_ = rpb_tables(np.asarray(inp["na_rpb"], f))
    m["rpbG"] = G_
    m["rpbM"] = M_
    m["sg_w_in"] = np.asarray(inp["sg_w_in"], f)
    bi_ = np.asarray(inp["sg_b_in"], f)
    m["sg_bu"] = np.ascontiguousarray(np.concatenate([col(bi_[s_, :3 * D]) for s_ in range(2)], axis=1))
    m["sg_bv"] = np.ascontiguousarray(bi_[:, 3 * D:])
    m["sg_lng"] = np.asarray(inp["sg_ln_g"], f)
    m["sg_lnb"] = np.asarray(inp["sg_ln_b"], f)
    ws = np.asarray(inp["sg_w_s"], f)
    m["sg_wsT"] = np.ascontiguousarray(ws.transpose(0, 3, 1, 2).reshape(2, 128, 24 * 128))
    m["sg_bs"] = np.ascontiguousarray(np.asarray(inp["sg_b_s"], f).reshape(2, 24 * 128))
    m["sg_w_o"] = np.asarray(inp["sg_w_o"], f)
    m["mlp_w1"] = np.asarray(inp["mlp_w1"], f)
    m["mlp_w2"] = np.asarray(inp["mlp_w2"], f)
    return m


def build_program():
    kb = K(nlt=NLT, nct=NCT)
    allt = list(range(NLT + NCT))
    latt = list(range(NLT))
    kb.phase_adaln()
    kb.phase_attn(0, first=True)
    kb.phase_mlp(0, tiles=allt)
    kb.phase_sg_a(1, tiles=allt)
    kb.phase_sg_b(1, tiles=allt)
    kb.phase_mlp(1, tiles=allt)
    kb.phase_attn(2)
    kb.phase_mlp(2, tiles=latt)
    kb.phase_sg_a(3, tiles=latt)
    kb.phase_sg_b(3, tiles=latt)
    kb.phase_mlp(3, last=True, tiles=latt)
    nc = kb.finish()
    return nc, kb


def kernel(**inputs):
    inp = {k: np.asarray(v) for k, v in inputs.items()}
    nc, _ = build_program()
    shared = host_inputs(inp, 0)
    in_maps = []
    for b in range(8):
        m = dict(shared)
        m.update(host_core_inputs(inp, b))
        in_maps.append(m)
    res = run_bass_kernel_spmd(nc, in_maps, core_ids=list(range(8)))
    return np.stack([np.asarray(res.results[b]["out"], dtype=np.float32) for b in range(8)], axis=0)
```
